# Optimizing a Trainium2 kernel written in Bass

```python
import math
import jax
import jax.numpy as jnp
from jax import lax
import numpy as np

D_MODEL = 1024
BATCH = 32
SEQ = 2048
DEPTH = 2

D_CONV = D_MODEL // 4
CONV_GROUPS = 4
CONV_WIDTH = 3
D_POOL = D_MODEL // 4
POOL_WINDOWS = (2, 4, 8, 16)
N_POOL = len(POOL_WINDOWS)
POOL_GROUP = D_POOL // N_POOL
D_ATTN = D_MODEL - D_CONV - D_POOL
HEAD_DIM = 64
N_HEADS = D_ATTN // HEAD_DIM
N_KV = 2
HPG = N_HEADS // N_KV
D_KV = N_KV * HEAD_DIM
D_MIX = D_CONV + D_ATTN + D_POOL

CMP_BLOCK = 32
CMP_STRIDE = 16
SEL_BLOCK = 64
TOP_N = 8
WINDOW = 512
N_BRANCH = 3
Q_BLOCK = 64

NUM_BUCKETS = 32
MAX_DISTANCE = 128

D_FF = ((8 * D_MODEL // 3 + 255) // 256) * 256
ALPHA = (2 * DEPTH) ** 0.25
BETA = (8 * DEPTH) ** -0.25
LN_EPS = 1e-5
NEG = -1e30
FORCE = 1e6

SPLIT_SIZES = (D_CONV, D_CONV, D_CONV, D_ATTN, D_KV, D_KV, D_KV, D_KV, D_KV, D_KV, N_BRANCH * N_HEADS, D_POOL)
D_IN = sum(SPLIT_SIZES)
SPLIT_POINTS = tuple(sum(SPLIT_SIZES[:i + 1]) for i in range(len(SPLIT_SIZES) - 1))

kernel_name = "hymba_style_conv_nsa_pool_hybrid"


def layer_norm(x, g, b):
    xf = x.astype(jnp.float32)
    mu = jnp.mean(xf, axis=-1, keepdims=True)
    var = jnp.mean(jnp.square(xf - mu), axis=-1, keepdims=True)
    return ((xf - mu) * lax.rsqrt(var + LN_EPS) * g + b).astype(x.dtype)


def t5_bucket(dist):
    n = jnp.maximum(dist, 0)
    max_exact = NUM_BUCKETS // 2
    nf = jnp.maximum(n, 1).astype(jnp.float32)
    large = max_exact + (jnp.log(nf / max_exact) / math.log(MAX_DISTANCE / max_exact)
                         * (NUM_BUCKETS - max_exact)).astype(jnp.int32)
    large = jnp.minimum(large, NUM_BUCKETS - 1)
    return jnp.where(n < max_exact, n, large)


def masked_softmax(s, valid):
    p = jax.nn.softmax(jnp.where(valid, s.astype(jnp.float32), NEG), axis=-1)
    return jnp.where(valid, p, 0.0)


def short_conv_mixer(b_gate, c_gate, x_conv, conv_w):
    u = c_gate * x_conv
    y = lax.conv_general_dilated(u, conv_w[:, None, :], window_strides=(1,),
                                 padding=[(CONV_WIDTH - 1, 0)],
                                 dimension_numbers=('NWC', 'WIO', 'NWC'),
                                 feature_group_count=D_CONV)
    return b_gate * y


def pool_mixer(u, pool_w, pool_scale):
    B, S, _ = u.shape
    uf = u.astype(jnp.float32)
    cs = jnp.pad(jnp.cumsum(uf, axis=1), ((0, 0), (1, 0), (0, 0)))
    t = jnp.arange(1, S + 1)
    groups = []
    for gi, w in enumerate(POOL_WINDOWS):
        sl = slice(gi * POOL_GROUP, (gi + 1) * POOL_GROUP)
        c = cs[..., sl]
        mean = (c[:, 1:] - jnp.take(c, jnp.maximum(t - w, 0), axis=1)) \
            / jnp.minimum(t, w).astype(jnp.float32)[:, None]
        groups.append(mean - uf[..., sl])
    d = jnp.stack(groups, axis=2).astype(u.dtype)
    y = jnp.einsum('bsgc,gcd->bsgd', d, pool_w) * pool_scale.reshape(N_POOL, POOL_GROUP)
    return y.reshape(B, S, D_POOL)


def nsa_mixer(q, k_cmp, v_cmp, k_slc, v_slc, k_win, v_win, gate_logits,
              cmp_pe, cmp_w1, cmp_w2, rel_bias):
    B, S, _ = q.shape
    nq = S // Q_BLOCK
    nc = (S - CMP_BLOCK) // CMP_STRIDE + 1
    nb = S // SEL_BLOCK
    n_sel = min(TOP_N, nb)
    scale = HEAD_DIM ** -0.5

    def heads_kv(t):
        return t.reshape(B, S, N_KV, HEAD_DIM).transpose(0, 2, 1, 3)

    def q_blocks(t, d):
        return t.reshape(B, nq, Q_BLOCK, N_KV, HPG, d).transpose(1, 0, 3, 4, 2, 5)

    qb_all = q_blocks(q, HEAD_DIM)
    gb_all = q_blocks(jax.nn.sigmoid(gate_logits), N_BRANCH)

    blk = np.arange(nc)[:, None] * CMP_STRIDE + np.arange(CMP_BLOCK)[None, :]

    def compress(kv, pe, w1, w2):
        z = (kv[:, :, blk] + pe).reshape(B, N_KV, nc, CMP_BLOCK * HEAD_DIM)
        return jnp.einsum('bgne,ed->bgnd', jax.nn.gelu(jnp.einsum('bgnf,fe->bgne', z, w1)), w2)

    kc = compress(heads_kv(k_cmp), cmp_pe[0], cmp_w1[0], cmp_w2[0])
    vc = compress(heads_kv(v_cmp), cmp_pe[1], cmp_w1[1], cmp_w2[1])
    cmp_start = np.arange(nc) * CMP_STRIDE
    cmp_end = jnp.asarray(cmp_start + CMP_BLOCK - 1, jnp.int32)

    sel_start = np.arange(nb) * SEL_BLOCK
    ov = np.clip(np.minimum(cmp_start[:, None] + CMP_BLOCK, sel_start[None, :] + SEL_BLOCK)
                 - np.maximum(cmp_start[:, None], sel_start[None, :]), 0, None) / CMP_STRIDE
    overlap = jnp.asarray(ov, jnp.float32)

    ks = heads_kv(k_slc).reshape(B, N_KV, nb, SEL_BLOCK, HEAD_DIM)
    vs = heads_kv(v_slc).reshape(B, N_KV, nb, SEL_BLOCK, HEAD_DIM)
    pad = ((0, 0), (0, 0), (WINDOW, 0), (0, 0))
    kw = jnp.pad(heads_kv(k_win), pad)
    vw = jnp.pad(heads_kv(v_win), pad)

    head_ids = jnp.arange(N_HEADS).reshape(N_KV, HPG)
    bias_flat = rel_bias.T.reshape(-1)

    def head_bias(dist):
        return rel_bias[t5_bucket(dist)].transpose(2, 0, 1).reshape(N_KV, HPG, *dist.shape)

    gather_blocks = jax.vmap(jax.vmap(lambda blocks, ix: blocks[ix]))

    def attend(args):
        qb, gb, s0 = args
        t = s0 + jnp.arange(Q_BLOCK)
        dist_c = t[:, None] - cmp_end[None, :]
        s_c = jnp.einsum('bghqd,bgnd->bghqn', qb, kc) * scale + head_bias(dist_c)
        p_c = masked_softmax(s_c, dist_c >= 0)
        o_c = jnp.einsum('bghqn,bgnd->bghqd', p_c.astype(vc.dtype), vc)
        imp = jnp.einsum('bghqn,nj->bgqj', p_c, overlap)
        cur = (t // SEL_BLOCK)[:, None]
        j = jnp.arange(nb)[None, :]
        forced = (j == 0) | (j == cur) | (j == cur - 1)
        imp = jnp.where(j * SEL_BLOCK <= t[:, None], imp + FORCE * forced, NEG)
        idx = lax.top_k(imp, n_sel)[1]
        kg = gather_blocks(ks, idx)
        vg = gather_blocks(vs, idx)
        dist_s = t[:, None, None] - (idx[..., None] * SEL_BLOCK + jnp.arange(SEL_BLOCK))
        bias_s = bias_flat[head_ids[None, :, :, None, None, None] * NUM_BUCKETS
                           + t5_bucket(dist_s)[:, :, None]]
        s_s = jnp.einsum('bghqd,bgqnkd->bghqnk', qb, kg) * scale + bias_s
        valid_s = (dist_s >= 0).reshape(B, N_KV, 1, Q_BLOCK, n_sel * SEL_BLOCK)
        p_s = masked_softmax(s_s.reshape(B, N_KV, HPG, Q_BLOCK, n_sel * SEL_BLOCK), valid_s)
        p_s = p_s.reshape(B, N_KV, HPG, Q_BLOCK, n_sel, SEL_BLOCK).astype(vg.dtype)
        o_s = jnp.einsum('bghqnk,bgqnkd->bghqd', p_s, vg)
        kwb = lax.dynamic_slice_in_dim(kw, s0, WINDOW + Q_BLOCK, axis=2)
        vwb = lax.dynamic_slice_in_dim(vw, s0, WINDOW + Q_BLOCK, axis=2)
        kpos = s0 - WINDOW + jnp.arange(WINDOW + Q_BLOCK)
        dist_w = t[:, None] - kpos[None, :]
        valid_w = (dist_w >= 0) & (dist_w < WINDOW) & (kpos[None, :] >= 0)
        s_w = jnp.einsum('bghqd,bgkd->bghqk', qb, kwb) * scale + head_bias(dist_w)
        p_w = masked_softmax(s_w, valid_w)
        o_w = jnp.einsum('bghqk,bgkd->bghqd', p_w.astype(vwb.dtype), vwb)
        return gb[..., 0:1] * o_c + gb[..., 1:2] * o_s + gb[..., 2:3] * o_w

    starts = jnp.arange(nq, dtype=jnp.int32) * Q_BLOCK
    o = lax.map(attend, (qb_all, gb_all, starts))
    return o.transpose(1, 0, 4, 2, 3, 5).reshape(B, S, D_ATTN)


def setup_inputs(seed: int = 0) -> dict:
    key = jax.random.key(seed)
    ks = jax.random.split(key, 17)
    f32 = jnp.float32

    def nrm(k, shape, scale):
        return jax.random.normal(k, shape, f32) * scale

    return {
        "x": nrm(ks[0], (BATCH, SEQ, D_MODEL), 1.0),
        "w_in": nrm(ks[1], (DEPTH, D_MODEL, D_IN), D_MODEL ** -0.5),
        "conv_w": nrm(ks[2], (DEPTH, CONV_WIDTH, D_CONV), CONV_WIDTH ** -0.5),
        "cmp_pe": nrm(ks[3], (DEPTH, 2, CMP_BLOCK, HEAD_DIM), 0.1),
        "cmp_w1": nrm(ks[4], (DEPTH, 2, CMP_BLOCK * HEAD_DIM, HEAD_DIM), (CMP_BLOCK * HEAD_DIM) ** -0.5),
        "cmp_w2": nrm(ks[5], (DEPTH, 2, HEAD_DIM, HEAD_DIM), HEAD_DIM ** -0.5),
        "pool_w": nrm(ks[6], (DEPTH, N_POOL, POOL_GROUP, POOL_GROUP), POOL_GROUP ** -0.5),
        "pool_scale": 1.0 + nrm(ks[7], (DEPTH, D_POOL), 0.1),
        "w_out": nrm(ks[8], (DEPTH, D_MIX, D_MODEL), BETA * D_MIX ** -0.5),
        "ln1_g": 1.0 + nrm(ks[9], (DEPTH, D_MODEL), 0.05),
        "ln1_b": nrm(ks[10], (DEPTH, D_MODEL), 0.02),
        "w_gate": nrm(ks[11], (DEPTH, D_MODEL, D_FF), D_MODEL ** -0.5),
        "w_up": nrm(ks[12], (DEPTH, D_MODEL, D_FF), D_MODEL ** -0.5),
        "w_down": nrm(ks[13], (DEPTH, D_FF, D_MODEL), BETA * D_FF ** -0.5),
        "ln2_g": 1.0 + nrm(ks[14], (DEPTH, D_MODEL), 0.05),
        "ln2_b": nrm(ks[15], (DEPTH, D_MODEL), 0.02),
        "rel_bias": nrm(ks[16], (NUM_BUCKETS, N_HEADS), 0.5),
    }


def reference(x, w_in, conv_w, cmp_pe, cmp_w1, cmp_w2, pool_w, pool_scale, w_out,
              ln1_g, ln1_b, w_gate, w_up, w_down, ln2_g, ln2_b, rel_bias):
    for l in range(DEPTH):
        h = jnp.einsum('bsd,de->bse', x, w_in[l])
        (b_gate, c_gate, x_conv, q, k_cmp, v_cmp, k_slc, v_slc, k_win, v_win,
         gate_logits, x_pool) = jnp.split(h, SPLIT_POINTS, axis=-1)
        y_a = short_conv_mixer(b_gate, c_gate, x_conv, conv_w[l])
        y_b = nsa_mixer(q, k_cmp, v_cmp, k_slc, v_slc, k_win, v_win, gate_logits,
                        cmp_pe[l], cmp_w1[l], cmp_w2[l], rel_bias)
        y_c = pool_mixer(x_pool, pool_w[l], pool_scale[l])
        mix = jnp.concatenate([y_a, y_b.astype(y_a.dtype), y_c], axis=-1)
        x = layer_norm(ALPHA * x + jnp.einsum('bsm,md->bsd', mix, w_out[l]), ln1_g[l], ln1_b[l])
        ffn = jnp.einsum('bsf,fd->bsd',
                         jax.nn.silu(jnp.einsum('bsd,df->bsf', x, w_gate[l]))
                         * jnp.einsum('bsd,df->bsf', x, w_up[l]), w_down[l])
        x = layer_norm(ALPHA * x + ffn, ln2_g[l], ln2_b[l])
    return x
```

```python
import os
import numpy as np
import concourse.bass as bass
import concourse.mybir as mybir
from concourse.bass_utils import run_bass_kernel_spmd

F32 = mybir.dt.float32
_KDBG = set(os.environ.get('KDBG', '').split(','))
BF16 = mybir.dt.bfloat16
AF = mybir.ActivationFunctionType
ALU = mybir.AluOpType

S = 2048
D = 1024
DFF = 2816
NFC = DFF // 128
D_IN = 2328
NCMP = 127
ALPHA = 4.0 ** 0.25
LN_EPS = 1e-5
NEGB = -30000.0
POOL_WINDOWS = (2, 4, 8, 16)
SLOT = 2048
NRING = 4
NH = 11

O_B, O_C, O_X, O_Q, O_KC, O_VC, O_KS, O_VS, O_KW, O_VW, O_G, O_P = 0, 256, 512, 768, 1280, 1408, 1536, 1664, 1792, 1920, 2048, 2072

FM_CHUNKS = [
    [(O_C, 128)], [(O_X, 128)], [(O_B, 128)],
    [(O_C + 128, 128)], [(O_X + 128, 128)], [(O_B + 128, 128)],
    [(O_P, 128)], [(O_P + 128, 128)],
]
for _g in range(2):
    FM_CHUNKS += [
        [(O_Q + 256 * _g, 128)], [(O_Q + 256 * _g + 128, 128)],
        [(O_KS + 64 * _g, 64), (O_KS + 64 * _g, 64)],
        [(O_KW + 64 * _g, 64), (O_KW + 64 * _g, 64)],
        [(O_KC + 64 * _g, 64), (O_VC + 64 * _g, 64)],
    ]
NFM = len(FM_CHUNKS)
TM_COLS = [[(O_VS + 64 * g, 64), (O_VW + 64 * g, 64), (O_G + 12 * g, 12)] for g in range(2)]
NTM = 140


def t5_bucket_np(dist):
    n = np.maximum(dist, 0)
    nf = np.maximum(n, 1).astype(np.float32)
    large = 16 + (np.log(nf / np.float32(16)) / np.float32(np.log(128 / 16)) * np.float32(16)).astype(np.int32)
    large = np.minimum(large, 31)
    return np.where(n < 16, n, large)


class Sem:
    def __init__(self, h):
        self.h = h
        self.total = 0


class Buf:
    __slots__ = ("name", "writer", "readers", "excl")

    def __init__(self, name, excl=False):
        self.name = name
        self.writer = None
        self.readers = {}
        self.excl = excl


class Eng:
    def __init__(self, name, h, sem, is_pe=False):
        self.name = name
        self.h = h
        self.sem = sem
        self.known = {}
        self.is_pe = is_pe
        self.ring = []
        self.ri = 0


class Prog:
    def __init__(self, nc, es):
        self.nc = nc
        self.es = es
        self.sems = []
        self.sb_off = 16640

    def new_sem(self, name):
        s = Sem(self.es.enter_context(self.nc.semaphore(name)))
        self.sems.append(s)
        return s

    def _waits(self, E, reads, writes):
        deps = []
        for b in reads:
            if b.writer is not None:
                deps.append(b.writer)
        for b in writes:
            if b.writer is not None:
                deps.append(b.writer)
            deps.extend(b.readers.items())
        need = {}
        for sem, val in deps:
            if E.is_pe and sem is E.sem:
                continue
            if E.known.get(sem, 0) >= val:
                continue
            if need.get(sem, 0) < val:
                need[sem] = val
        for sem, val in need.items():
            E.h.wait_ge(sem.h, val)
            E.known[sem] = val

    def _mark(self, tok, reads, writes):
        sem, val = tok
        for b in writes:
            b.writer = tok
            b.readers = {}
        for b in reads:
            if b.readers.get(sem, 0) < val:
                b.readers[sem] = val

    @staticmethod
    def _split(reads, writes):
        ex = [b for b in reads if b.excl]
        if ex:
            reads = [b for b in reads if not b.excl]
            writes = list(writes) + [b for b in ex if b not in writes]
        return reads, writes

    def op(self, E, fn, reads=(), writes=()):
        reads, writes = self._split(reads, writes)
        self._waits(E, reads, writes)
        ins = fn(E.h)
        E.sem.total += 1
        ins.then_inc(E.sem.h, 1)
        self._mark((E.sem, E.sem.total), reads, writes)

    def dma(self, Q, out, in_, reads=(), writes=()):
        reads, writes = self._split(reads, writes)
        self._waits(Q, reads, writes)
        s = Q.ring[Q.ri]
        Q.ri = (Q.ri + 1) % len(Q.ring)
        if Q.known.get(s, 0) < s.total:
            Q.h.wait_ge(s.h, s.total)
            Q.known[s] = s.total
        Q.h.dma_start(out=out, in_=in_).then_inc(s.h, 16)
        s.total += 16
        self._mark((s, s.total), reads, writes)

    def barrier(self, engines, skip=()):
        for E in engines:
            for s in self.sems:
                if s in skip:
                    continue
                if s.total > E.known.get(s, 0) and not (s is E.sem):
                    E.h.wait_ge(s.h, s.total)
                    E.known[s] = s.total
            if E.sem is not None and not E.is_pe and E.sem.total > E.known.get(E.sem, 0):
                E.h.wait_ge(E.sem.h, E.sem.total)
                E.known[E.sem] = E.sem.total

    def sb(self, name, shape, dt, off=None):
        nbytes = int(np.prod(shape[1:])) * (4 if dt == F32 else 2)
        nbytes = (nbytes + 63) // 64 * 64
        if off is None:
            off = self.sb_off
            self.sb_off += nbytes
        t = self.nc.alloc_sbuf_tensor_at(name, list(shape), dt, offset=off)
        return t, off + nbytes


class _Stop(Exception):
    pass


def build_nc(nseq=4, layers=(0, 1), debug=False, stop_after=None):
    from contextlib import ExitStack
    nc = bass.Bass("TRN2", target_bir_lowering=False)
    NL = 2

    def din(name, shape, dt=F32):
        return nc.dram_tensor(name, list(shape), dt, kind="ExternalInput").ap()

    def dsc(name, shape, dt=BF16):
        return nc.dram_tensor(name, list(shape), dt, kind="Internal").ap()

    x_d = din("x", [nseq, S, D])
    w_in_d = din("w_in", [NL, D, D_IN])
    convw_d = din("convw_t", [128, NL * 2 * 3])
    pet_d = din("pet_t", [128, NL * 32])
    lnp_d = din("lnp_t", [128, NL * 4 * 8])
    pscale_d = din("pscale_t", [128, NL * 2])
    cmp_w1_d = din("cmp_w1", [NL, 2, 2048, 64])
    cmp_w2_d = din("cmp_w2", [NL, 2, 64, 64])
    pool_w_d = din("pool_w", [NL, 4, 64, 64])
    w_out_d = din("w_out", [NL, D, D])
    w_gate_d = din("w_gate", [NL, D, DFF])
    w_up_d = din("w_up", [NL, D, DFF])
    w_down_d = din("w_down", [NL, DFF, D])
    strips_d = din("strips", [8, 128, 1024])
    cmpb_d = din("cmpb", [8, 128, S])
    b31_d = din("b31", [128, 8])
    cf32_d = din("cf32", [128, 128 + 128 + 512 + 32 + 2 + 2])
    cbf_d = din("cbf", [128, 128 + 896 + 33 + S + 128])
    y_d = nc.dram_tensor("y", [nseq, S, D], F32, kind="ExternalOutput").ap()
    dbg_d = None
    if debug:
        dbg_d = nc.dram_tensor("dbg", [128, 8 * S], BF16, kind="ExternalOutput").ap()

    win_sc = dsc("win_sc", [NL, NFM, 128, 1024])
    wtm_sc = dsc("wtm_sc", [NL, 2, 128, 8 * NTM])
    wout_sc = dsc("wout_sc", [NL, 8, 128, 1024])
    wgu_sc = dsc("wgu_sc", [NL, NFC, 128, 2048])
    wdn_sc = dsc("wdn_sc", [NL, 8, 128, NFC * 128])
    w1_sc = dsc("w1_sc", [NL, 2, 128, 2048])
    strips_sc = dsc("strips_sc", [8, 128, 1024])
    cmpb_sc = dsc("cmpb_sc", [8, 128, S])

    with ExitStack() as es:
        pg = Prog(nc, es)
        PE = Eng("pe", nc.tensor, pg.new_sem("s_pe"), is_pe=True)
        ACT = Eng("act", nc.scalar, pg.new_sem("s_act"))
        DVE = Eng("dve", nc.vector, pg.new_sem("s_dve"))
        POOL = Eng("pool", nc.gpsimd, pg.new_sem("s_pool"))
        SP = Eng("sp", nc.sync, None)
        SP.ring = [pg.new_sem(f"s_spd{i}") for i in range(10)]
        POOL.ring = [pg.new_sem(f"s_pld{i}") for i in range(8)]
        ENGS = [PE, ACT, DVE, POOL, SP]

        PSA = nc.alloc_psum_tensor("psa", [128, 2048], F32)
        PSB = nc.alloc_psum_tensor("psb", [128, 2048], F32)
        banks = []
        for i in range(8):
            T = PSA if i < 4 else PSB
            j = i % 4
            banks.append((T, j * 512, Buf(f"bank{i}", excl=True)))
        bank_rr = [0]

        def next_bank():
            b = banks[bank_rr[0] % 6]
            bank_rr[0] += 1
            return b

        obank_rr = [0]

        def next_obank():
            b = banks[6 + obank_rr[0] % 2]
            obank_rr[0] += 1
            return b

        def bank4(which):
            T = PSA if which == 0 else PSB
            bs = [banks[which * 4 + j][2] for j in range(4)]
            return T, bs

        def sbp(name, shape, dt):
            t, _ = pg.sb(name, shape, dt)
            return t

        XH = sbp("XH", [128, 8, S], BF16)
        XL = sbp("XL", [128, 8, S], BF16)
        MIX = sbp("MIX", [128, 8, S], BF16)
        bXH = [[Buf(f"xh{c}_{tb}") for tb in range(4)] for c in range(8)]
        bXL = [[Buf(f"xl{c}_{tb}") for tb in range(4)] for c in range(8)]
        bMIX = [[Buf(f"mix{c}_{tb}") for tb in range(4)] for c in range(8)]
        RING = [sbp(f"ring{i}", [128, SLOT], BF16) for i in range(NRING)]
        bRING = [Buf(f"ring{i}") for i in range(NRING)]
        CF = sbp("CF", [128, 128 + 128 + 512 + 32 + 4], F32)
        bCF = Buf("cf")
        IDF = CF[:, 0:128]
        ONESF = CF[:, 128:256]
        ADDM = CF[:, 256:768]
        RC16 = CF[:, 768:800]
        INVW = CF[:, 800:802]
        CB_ = sbp("CBF", [128, 128 + 896 + 33 + S + 128], BF16)
        bCB = Buf("cbf")
        IDB = CB_[:, 0:128]
        CUT = CB_[:, 128:1024]
        ONEOV = CB_[:, 1024:1057]
        EALL = CB_[:, 1057:1057 + S]
        ONESB = CB_[:, 1057 + S:1057 + S + 128]
        B31 = sbp("B31", [128, 8], F32)
        LNP = sbp("LNP", [128, NL, 4, 8], F32)
        CONVW = sbp("CONVW", [128, NL, 2, 3], F32)
        PSCALE = sbp("PSCALE", [128, NL, 2], F32)
        POOLBD = sbp("POOLBD", [128, NL, 2, 128], BF16)
        W2K = sbp("W2K", [64, NL, 128], BF16)
        W2V = sbp("W2V", [64, NL, 64], BF16)
        PET = sbp("PET", [128, NL, 32], BF16)
        VCA = sbp("VCA", [128, 97], BF16)
        ZT = sbp("ZT", [128, 512], BF16)
        bZT = Buf("zt")
        bPARAM = Buf("params")
        bVCA = Buf("vca")
        UBASE = pg.sb_off

        def union_alloc(specs):
            off = UBASE
            out = {}
            for name, shape, dt in specs:
                t, off = pg.sb(name, shape, dt, off=off)
                out[name] = t
            return out, off

        u0, e0 = union_alloc([("XST%d" % i, [128, 1024], F32) for i in range(4)])
        u1, e1 = union_alloc([("T0", [128, 2064], F32), ("T1", [128, 2064], F32), ("T2", [128, 2064], F32),
                              ("DT", [128, S], BF16), ("T16", [128, 16], F32)])
        u2, e2 = union_alloc([
            ("QH0", [128, S], BF16), ("QH1", [128, S], BF16), ("QH2", [128, S], BF16), ("QH3", [128, S], BF16), ("QTMP", [128, S], BF16), ("KSE", [128, S], BF16), ("KWZ", [128, S], BF16), ("KVT", [128, S], BF16),
            ("VS", [128, 16, 66], BF16), ("VW", [128, 16, 66], BF16), ("GATES", [128, 16, 12], F32),
            ("KCZ", [128, 128], BF16), ("CU", [64, 128], F32), ("CU2", [64, 128], F32), ("CSG", [64, 128], F32),
            ("CST", [64, 2], F32), ("GT", [64, 128], BF16),
            ("IMP", [128, 16, 32], F32), ("SCORE", [128, 16, 32], F32), ("M8", [128, 16, 8], F32),
            ("NEGM", [128, 16, 96], F32), ("ACC", [128, 16, 4, 64], F32),
            ("CBT0", [128, 512], BF16), ("CBT1", [128, 512], BF16),
            ("PT0", [128, 512], BF16), ("PT1", [128, 512], BF16), ("PT2", [128, 512], BF16), ("PT3", [128, 512], BF16),
            ("ST0", [128, 1024], BF16), ("ST1", [128, 1024], BF16),
            ("RL", [128, 4], F32), ("CG", [128, 4], F32), ("FT", [128, 4, 64], F32), ("FT2", [128, 4, 32], F32),
        ])
        u3, e3 = union_alloc([
            ("Z", [128, 8, 512], F32), ("ZB0", [128, 512], BF16), ("ZB1", [128, 512], BF16), ("ZQ0", [128, 512], BF16), ("ZQ1", [128, 512], BF16),
            ("FACT", [128, NFC, 512], BF16), ("SIL0", [128, 512], F32), ("SIL1", [128, 512], F32),
            ("MEAN", [128, 512], F32), ("RSTD", [128, 512], F32), ("M2", [128, 512], F32),
            ("NT0", [128, 512], F32), ("NT1", [128, 512], F32), ("NX0", [128, 512], F32), ("NX1", [128, 512], F32),
            ("OST0", [128, 1024], F32), ("OST1", [128, 1024], F32),
        ])
        assert max(e0, e1, e2, e3) <= 229376, (e0, e1, e2, e3)
        print('sbuf union ends', e0, e1, e2, e3)
        ubuf = {}

        def UB(name):
            if name not in ubuf:
                ubuf[name] = Buf(name)
            return ubuf[name]

        def full_barrier():
            pg.barrier(ENGS, skip=POOL.ring)

        pg.dma(SP, CF[:, :], cf32_d[:, :], writes=[bCF])
        pg.dma(SP, B31[:, :], b31_d[:, :], writes=[bPARAM])
        pg.dma(SP, LNP[:, :, :, :].rearrange("p a b c -> p (a b c)"), lnp_d[:, :], writes=[bPARAM])
        pg.dma(SP, CONVW[:, :, :, :].rearrange("p a b c -> p (a b c)"), convw_d[:, :], writes=[bPARAM])
        pg.dma(SP, PSCALE[:, :, :].rearrange("p a b -> p (a b)"), pscale_d[:, :], writes=[bPARAM])
        pg.dma(POOL, PET[:, :, :].rearrange("p a b -> p (a b)"), pet_d[:, :], writes=[bPARAM])
        pg.op(POOL, lambda e: e.memset(POOLBD[:, :, :, :], 0.0), writes=[bPARAM])
        pg.op(POOL, lambda e: e.memset(ZT[:, :], 0.0), writes=[bZT])
        pg.dma(POOL, CB_[:, :], cbf_d[:, :], writes=[bCB])
        for l in range(NL):
            for ch in range(2):
                for hf in range(2):
                    pg.dma(POOL, POOLBD[hf * 64:(hf + 1) * 64, l, ch, hf * 64:(hf + 1) * 64],
                           pool_w_d[l, 2 * ch + hf], writes=[bPARAM])
            for dup in range(2):
                pg.dma(POOL, W2K[:, l, dup * 64:(dup + 1) * 64], cmp_w2_d[l, 0], writes=[bPARAM])
            pg.dma(POOL, W2V[:, l, :], cmp_w2_d[l, 1], writes=[bPARAM])
        pg.op(POOL, lambda e: e.tensor_copy(out=VCA[:, 64:97], in_=ONEOV), reads=[bCB], writes=[bVCA])

        bsc = {}

        def SCB(key):
            if key not in bsc:
                bsc[key] = Buf(str(key))
            return bsc[key]

        pending_casts = []
        cast_done = set()

        def qcast(key, out, in_):
            pending_casts.append((key, out, in_))

        def _emit_one_key():
            key = pending_casts[0][0]
            while pending_casts and pending_casts[0][0] == key:
                _, out, in_ = pending_casts.pop(0)
                pg.dma(POOL, out, in_, writes=[SCB(key)])
            cast_done.add(key)

        def ensure_cast(key):
            while key not in cast_done:
                assert pending_casts, key
                _emit_one_key()

        def drip(n):
            for _ in range(n):
                if pending_casts:
                    _emit_one_key()

        def cast_layer(l, part):
            if part == 0:
                cast_layer_a(l)
            else:
                cast_layer_b(l)

        def cast_layer_a(l):
            for ci, pieces in enumerate(FM_CHUNKS):
                off = 0
                for (c0, n) in pieces:
                    qcast(("win", l, ci), win_sc[l, ci].rearrange("p (k c) -> p k c", k=8)[:, :, off:off + n],
                           w_in_d[l, :, c0:c0 + n].rearrange("(k p) c -> p k c", p=128))
                    off += n
                if ci == 7:
                    pass
            for g in range(2):
                off = 0
                for (c0, n) in TM_COLS[g]:
                    qcast(("wtm", l, g), wtm_sc[l, g].rearrange("p (k c) -> p k c", k=8)[:, :, off:off + n],
                           w_in_d[l, :, c0:c0 + n].rearrange("(k p) c -> p k c", p=128))
                    off += n
            for kv in range(2):
                for dup in range(2):
                    qcast(("w1", l, kv), w1_sc[l, kv, dup * 64:(dup + 1) * 64, :].rearrange("d (j e) -> d j e", j=32),
                           cmp_w1_d[l, kv].rearrange("(j d) e -> d j e", d=64))

        def cast_layer_b(l):
            for ec in range(8):
                qcast(("wout", l, ec), wout_sc[l, ec].rearrange("p (k c) -> p k c", k=8),
                       w_out_d[l, :, ec * 128:(ec + 1) * 128].rearrange("(k p) c -> p k c", p=128))
            for fc in range(NFC):
                v = wgu_sc[l, fc].rearrange("p (k c) -> p k c", k=8)
                qcast(("wgu", l, fc), v[:, :, 0:128], w_gate_d[l, :, fc * 128:(fc + 1) * 128].rearrange("(k p) c -> p k c", p=128))
                qcast(("wgu", l, fc), v[:, :, 128:256], w_up_d[l, :, fc * 128:(fc + 1) * 128].rearrange("(k p) c -> p k c", p=128))
            for ec in range(8):
                qcast(("wdn", l, ec), wdn_sc[l, ec].rearrange("p (k c) -> p k c", k=NFC),
                       w_down_d[l, :, ec * 128:(ec + 1) * 128].rearrange("(k p) c -> p k c", p=128))

        for l in layers:
            cast_layer(l, 0)
            if l == layers[0]:
                for h in range(8):
                    qcast(("cmpb", h), cmpb_sc[h], cmpb_d[h])
                for h in range(8):
                    qcast(("strip", h), strips_sc[h], strips_d[h])
            cast_layer(l, 1)

        ring_i = [0]

        def ring_load(src_ap, nel, key):
            i = ring_i[0] % NRING
            ring_i[0] += 1
            ensure_cast(key)
            pg.dma(SP, RING[i][:, 0:nel], src_ap, reads=[SCB(key)], writes=[bRING[i]])
            return RING[i], bRING[i]

        class Stream:
            def __init__(self, items, depth=NRING):
                self.items = items
                self.depth = depth
                self.loaded = []
                self.n = 0

            def _issue(self):
                src, nel, key = self.items[self.n]
                self.loaded.append(ring_load(src, nel, key))
                self.n += 1

            def get(self, i):
                while self.n <= min(i + self.depth - 1, len(self.items) - 1) or self.n <= i:
                    self._issue()
                return self.loaded[i]

        class Pref:
            def __init__(self, items, bufs):
                self.items, self.bufs, self.n, self.out = items, bufs, 0, []

            def get(self, i):
                while self.n < len(self.items) and self.n <= i + len(self.bufs) - 1:
                    src_ap, key = self.items[self.n]
                    t, b = self.bufs[self.n % len(self.bufs)]
                    ensure_cast(key)
                    pg.dma(SP, t[:, :], src_ap, reads=[SCB(key)], writes=[b])
                    self.out.append((t, b))
                    self.n += 1
                return self.out[i]

        mm = lambda out, lhsT, rhs, st, sp: (lambda e: e.matmul(out, lhsT=lhsT, rhs=rhs, start=st, stop=sp))

        def chk(name):
            if stop_after == name:
                raise _Stop()

        try:
          if debug:
              pg.op(POOL, lambda e: e.memset(MIX[:, :, :], 0.0), writes=[b for c in bMIX for b in c])
          chk("prologue")
          for s in range(nseq):
              for tt in range(16):
                  xs = u0["XST%d" % (tt % 4)]
                  bx = UB("XST%d" % (tt % 4))
                  pg.dma(SP, xs[:, :], x_d[s, tt * 128:(tt + 1) * 128, :], writes=[bx])
                  for hf in range(2):
                      T, o, bb = next_bank()
                      for j in range(4):
                          dc = hf * 4 + j
                          pg.op(PE, lambda e, T=T, o=o, j=j, dc=dc, xs=xs: e.transpose(
                              T[:, o + j * 128:o + (j + 1) * 128], xs[:, dc * 128:(dc + 1) * 128], IDF),
                              reads=[bx, bCF], writes=[bb])
                      pv = T[:, o:o + 512].rearrange("p (j t) -> p j t", j=4)
                      xh = XH[:, hf * 4:hf * 4 + 4, tt * 128:(tt + 1) * 128]
                      xl = XL[:, hf * 4:hf * 4 + 4, tt * 128:(tt + 1) * 128]
                      wb = [bXH[hf * 4 + j][tt // 4] for j in range(4)]
                      wl = [bXL[hf * 4 + j][tt // 4] for j in range(4)]
                      pg.op(ACT, lambda e, pv=pv, xh=xh: e.activation(out=xh, in_=pv, func=AF.Copy), reads=[bb], writes=wb)
                      pg.op(DVE, lambda e, pv=pv, xh=xh, xl=xl: e.tensor_tensor(out=xl, in0=pv, in1=xh, op=ALU.subtract),
                            reads=[bb] + wb, writes=wl)
              full_barrier()
              chk('phase0')

              for li, l in enumerate(layers):
                  last_layer = (li == len(layers) - 1)
                  allXH = [b for c in bXH for b in c]
                  items = []
                  for ci in range(8):
                      items.append((win_sc[l, ci], 1024, ("win", l, ci)))
                  for g in range(2):
                      for k in range(5):
                          ci = 8 + 5 * g + k
                          items.append((win_sc[l, ci], 1024, ("win", l, ci)))
                      items.append((wtm_sc[l, g], 8 * NTM, ("wtm", l, g)))
                      items.append((w1_sc[l, 0], 2048, ("w1", l, 0)))
                      items.append((w1_sc[l, 1], 2048, ("w1", l, 1)))
                  for tb in range(4):
                      for ec in range(8):
                          items.append((wout_sc[l, ec], 1024, ("wout", l, ec)))
                  for tb in range(4):
                      for fc in range(NFC):
                          items.append((wgu_sc[l, fc], 2048, ("wgu", l, fc)))
                      for ec in range(8):
                          items.append((wdn_sc[l, ec][:, 0:NH * 128], NH * 128, ("wdn", l, ec)))
                          items.append((wdn_sc[l, ec][:, NH * 128:NFC * 128], NH * 128, ("wdn", l, ec)))
                  st = Stream(items)
                  si = [0]

                  def next_slab():
                      drip(2)
                      r = st.get(si[0])
                      si[0] += 1
                      return r

                  def proj_fm(which, m=128):
                      slab, bs = next_slab()
                      T, bks = bank4(which)
                      for kc in range(8):
                          for tb in range(4):
                              pg.op(PE, mm(T[0:m, tb * 512:(tb + 1) * 512], slab[:, kc * 128:kc * 128 + m],
                                           XH[:, kc, tb * 512:(tb + 1) * 512], kc == 0, kc == 7),
                                    reads=[bs, bXH[kc][tb]], writes=[bks[tb]])
                      return T, bks

                  T0, T1, T2, DT, T16 = u1["T0"], u1["T1"], u1["T2"], u1["DT"], u1["T16"]
                  bT0, bT1, bT2, bDT, bT16 = UB("T0"), UB("T1"), UB("T2"), UB("DT"), UB("T16")
                  for t, b in ((T0, bT0), (T1, bT1), (T2, bT2)):
                      pg.op(POOL, lambda e, t=t: e.memset(t[:, 0:16], 0.0), writes=[b])
                  wh = 0
                  for ch in range(2):
                      T, bks = proj_fm(wh); wh ^= 1
                      pg.op(ACT, lambda e, T=T: e.activation(out=T0[:, 16:16 + S], in_=T[:, :], func=AF.Copy),
                            reads=bks, writes=[bT0])
                      T, bks = proj_fm(wh); wh ^= 1
                      pg.op(DVE, lambda e, T=T: e.tensor_tensor(out=T1[:, 16:16 + S], in0=T0[:, 16:16 + S], in1=T[:, :],
                                                              op=ALU.mult), reads=bks + [bT0], writes=[bT1])
                      cw = lambda k: CONVW[:, l, ch, k:k + 1]
                      pg.op(ACT, lambda e: e.activation(out=T2[:, 16:16 + S], in_=T1[:, 14:14 + S], func=AF.Copy,
                                                        scale=cw(0)), reads=[bT1, bPARAM], writes=[bT2])
                      pg.op(DVE, lambda e: e.scalar_tensor_tensor(out=T2[:, 16:16 + S], in0=T1[:, 15:15 + S], scalar=cw(1),
                                                                  in1=T2[:, 16:16 + S], op0=ALU.mult, op1=ALU.add),
                            reads=[bT1, bT2, bPARAM], writes=[bT2])
                      pg.op(DVE, lambda e: e.scalar_tensor_tensor(out=T2[:, 16:16 + S], in0=T1[:, 16:16 + S], scalar=cw(2),
                                                                  in1=T2[:, 16:16 + S], op0=ALU.mult, op1=ALU.add),
                            reads=[bT1, bT2, bPARAM], writes=[bT2])
                      T, bks = proj_fm(wh); wh ^= 1
                      pg.op(DVE, lambda e, T=T: e.tensor_tensor(out=MIX[:, ch, :], in0=T2[:, 16:16 + S], in1=T[:, :],
                                                              op=ALU.mult), reads=bks + [bT2], writes=bMIX[ch])
                  for ch in range(2):
                      T, bks = proj_fm(wh); wh ^= 1
                      pg.op(ACT, lambda e, T=T: e.activation(out=T0[:, 16:16 + S], in_=T[:, :], func=AF.Copy),
                            reads=bks, writes=[bT0])
                      sh = lambda t, k: t[:, 16 - k:16 - k + S]
                      pg.op(DVE, lambda e: e.tensor_tensor(out=T1[:, 16:16 + S], in0=T0[:, 16:16 + S], in1=sh(T0, 1), op=ALU.add),
                            reads=[bT0], writes=[bT1])
                      if ch == 0:
                          pg.op(DVE, lambda e: e.tensor_tensor(out=T2[64:128, 16:16 + S], in0=T1[64:128, 16:16 + S],
                                                               in1=T1[64:128, 14:14 + S], op=ALU.add),
                                reads=[bT1], writes=[bT2])
                          lo_src, hi_src = T1, T2
                      else:
                          pg.op(DVE, lambda e: e.tensor_tensor(out=T2[:, 16:16 + S], in0=T1[:, 16:16 + S], in1=sh(T1, 2),
                                                               op=ALU.add), reads=[bT1], writes=[bT2])
                          pg.op(DVE, lambda e: e.tensor_tensor(out=T1[:, 16:16 + S], in0=T2[:, 16:16 + S], in1=sh(T2, 4),
                                                               op=ALU.add), reads=[bT2], writes=[bT1])
                          pg.op(DVE, lambda e: e.tensor_tensor(out=T2[64:128, 16:16 + S], in0=T1[64:128, 16:16 + S],
                                                               in1=T1[64:128, 8:8 + S], op=ALU.add),
                                reads=[bT1, bT2], writes=[bT2])
                          lo_src, hi_src = T1, T2
                      for (r0, src) in ((0, lo_src), (64, hi_src)):
                          pg.op(DVE, lambda e, r0=r0, src=src: e.scalar_tensor_tensor(
                              out=DT[r0:r0 + 64, :], in0=src[r0:r0 + 64, 16:16 + S], scalar=INVW[r0:r0 + 64, ch:ch + 1],
                              in1=T0[r0:r0 + 64, 16:16 + S], op0=ALU.mult, op1=ALU.subtract),
                              reads=[bT0, bT1, bT2, bCF], writes=[bDT])
                          pg.op(DVE, lambda e, r0=r0, src=src: e.tensor_tensor(
                              out=T16[r0:r0 + 64, :], in0=src[r0:r0 + 64, 16:32], in1=RC16[r0:r0 + 64, ch * 16:(ch + 1) * 16],
                              op=ALU.mult), reads=[bT1, bT2, bCF], writes=[bT16])
                          pg.op(DVE, lambda e, r0=r0: e.tensor_tensor(
                              out=DT[r0:r0 + 64, 0:16], in0=T16[r0:r0 + 64, :], in1=T0[r0:r0 + 64, 16:32], op=ALU.subtract),
                              reads=[bT16, bT0], writes=[bDT])
                      T, bks = bank4(wh); wh ^= 1
                      for tb in range(4):
                          pg.op(PE, mm(T[:, tb * 512:(tb + 1) * 512], POOLBD[:, l, ch, :], DT[:, tb * 512:(tb + 1) * 512], True, True),
                                reads=[bDT, bPARAM], writes=[bks[tb]])
                      pg.op(ACT, lambda e, T=T: e.activation(out=MIX[:, 6 + ch, :], in_=T[:, :], func=AF.Copy,
                                                             scale=PSCALE[:, l, ch:ch + 1]),
                            reads=bks + [bPARAM], writes=bMIX[6 + ch])
                  full_barrier()
                  chk('phase1a')

                  U = u2
                  QH, QTMP, KSE, KWZ, KVT, VS, VW, GATES = [U["QH%d" % i] for i in range(4)], U["QTMP"], U["KSE"], U["KWZ"], U["KVT"], U["VS"], U["VW"], U["GATES"]
                  KCZ, CU, CU2, CSG, CST, GT = U["KCZ"], U["CU"], U["CU2"], U["CSG"], U["CST"], U["GT"]
                  IMP, SCORE, M8, NEGM, ACC = U["IMP"], U["SCORE"], U["M8"], U["NEGM"], U["ACC"]
                  CBT = [U["CBT0"], U["CBT1"]]
                  PTS = [U["PT0"], U["PT1"], U["PT2"], U["PT3"]]
                  STS = [U["ST0"], U["ST1"]]
                  RL, CG, FT, FT2 = U["RL"], U["CG"], U["FT"], U["FT2"]
                  bQH = [UB("QH%d" % i) for i in range(4)]
                  pt_i = [0]
                  cb_i = [0]
                  st_i = [0]
                  for g in range(2):
                      for hh in range(4):
                          pg.op(POOL, lambda e, hh=hh: e.memset(QH[hh][64:128, :], 0.0), writes=[bQH[hh]])
                          pg.op(ACT, lambda e, hh=hh: e.activation(out=QH[hh][96:97, :], in_=QH[hh][96:97, :], func=AF.Identity,
                                                                  bias=B31[96:97, 4 * g + hh:4 * g + hh + 1]),
                                reads=[bQH[hh], bPARAM], writes=[bQH[hh]])
                      pg.op(POOL, lambda e: e.memset(KSE[96:128, :], 0.0), writes=[UB("KST")])
                      pg.op(POOL, lambda e: e.memset(KSE[96:97, :], 1.0), writes=[UB("KST")])
                      pg.op(POOL, lambda e: e.tensor_copy(out=KSE[64:96, :], in_=EALL[64:96, :]), reads=[bCB], writes=[UB("KST")])
                      pg.op(POOL, lambda e: e.memset(KWZ[64:128, :], 0.0), writes=[UB("KWT")])
                      pg.op(POOL, lambda e: e.memset(KWZ[96:97, :], 1.0), writes=[UB("KWT")])
                      pg.op(POOL, lambda e: e.memset(KCZ[64:128, :], 0.0), writes=[UB("KCT")])
                      pg.op(POOL, lambda e: e.memset(NEGM[:, :, 0:64], 0.0), writes=[UB("SEL")])
                      for j in range(2):
                          T, bks = proj_fm(wh); wh ^= 1
                          pg.op(ACT, lambda e, T=T, j=j: e.activation(out=QH[2 * j][0:64, :], in_=T[0:64, :], func=AF.Copy, scale=0.125),
                                reads=bks, writes=[bQH[2 * j]])
                          pg.op(DVE, lambda e, T=T: e.tensor_scalar(out=QTMP[64:128, :], in0=T[64:128, :], scalar1=0.125, scalar2=None,
                                                                   op0=ALU.mult), reads=bks, writes=[UB("QTMP")])
                          pg.dma(SP, QH[2 * j + 1][0:64, :], QTMP[64:128, :], reads=[UB("QTMP")], writes=[bQH[2 * j + 1]])
                      T, bks = proj_fm(wh, 64); wh ^= 1
                      pg.op(DVE, lambda e, T=T: e.tensor_copy(out=KSE[0:64, :], in_=T[0:64, :]), reads=bks, writes=[UB("KST")])
                      T, bks = proj_fm(wh, 64); wh ^= 1
                      pg.op(ACT, lambda e, T=T: e.activation(out=KWZ[0:64, :], in_=T[0:64, :], func=AF.Copy), reads=bks, writes=[UB("KWT")])
                      T, bks = proj_fm(wh); wh ^= 1
                      pg.op(DVE, lambda e, T=T: e.tensor_copy(out=KVT[:, :], in_=T[:, :]), reads=bks, writes=[UB("KVT")])
                      chk('projfm')
                      slab, bs = next_slab()
                      if 'nomemset' not in _KDBG:
                          pg.op(POOL, lambda e: e.memset(VS[:, :, 64:65], 1.0), writes=[UB("VS")])
                          pg.op(POOL, lambda e: e.memset(VW[:, :, 64:65], 1.0), writes=[UB("VW")])
                      for tp in range(8):
                          T, o, bb = next_bank()
                          for k2 in range(2):
                              tt = tp * 2 + k2
                              for kc in range(8):
                                  pg.op(PE, mm(T[:, o + k2 * NTM:o + (k2 + 1) * NTM], XH[:, kc, tt * 128:(tt + 1) * 128],
                                               slab[:, kc * NTM:(kc + 1) * NTM], kc == 0, kc == 7),
                                        reads=[bs, bXH[kc][tt // 4]], writes=[bb])
                          pv = T[:, o:o + 2 * NTM].rearrange("p (k c) -> p k c", k=2)
                          if 'noevac' in _KDBG:
                              continue
                          if 'noev1' not in _KDBG:
                              pg.op(ACT, lambda e, pv=pv, tp=tp: e.activation(out=VS[:, 2 * tp:2 * tp + 2, 0:64], in_=pv[:, :, 0:64],
                                                                           func=AF.Copy), reads=[bb], writes=[UB("VS")])
                          if 'noev2' not in _KDBG:
                              pg.op(DVE, lambda e, pv=pv, tp=tp: e.tensor_copy(out=VW[:, 2 * tp:2 * tp + 2, 0:64], in_=pv[:, :, 64:128]),
                                    reads=[bb], writes=[UB("VW")])
                          if 'noev3' not in _KDBG:
                              pg.op(ACT, lambda e, pv=pv, tp=tp: e.activation(out=GATES[:, 2 * tp:2 * tp + 2, :], in_=pv[:, :, 128:140],
                                                                           func=AF.Sigmoid), reads=[bb], writes=[UB("GATES")])
                      chk('projtm')
                      for kv in range(2):
                          slab, bs = next_slab()
                          r0 = 64 * kv
                          T, o, bb = next_bank()
                          for j in range(32):
                              pg.op(PE, mm(T[0:64, o:o + NCMP], slab[r0:r0 + 64, j * 64:(j + 1) * 64],
                                           KVT[r0:r0 + 64, j:j + 16 * (NCMP - 1) + 1:16], j == 0, j == 31),
                                    reads=[bs, UB("KVT")], writes=[bb])
                          for j in range(32):
                              pg.op(PE, mm(T[0:64, o + 128:o + 129], slab[r0:r0 + 64, j * 64:(j + 1) * 64],
                                           PET[r0:r0 + 64, l, j:j + 1], j == 0, j == 31),
                                    reads=[bs, bPARAM], writes=[bb])
                          bC = UB("CMPT")
                          pg.op(DVE, lambda e, T=T, o=o: e.tensor_copy(out=CST[:, 0:1], in_=T[0:64, o + 128:o + 129]),
                                reads=[bb], writes=[bC])
                          pg.op(DVE, lambda e, T=T, o=o: e.tensor_scalar(out=CU[:, 0:NCMP], in0=T[0:64, o:o + NCMP], scalar1=CST[:, 0:1],
                                                                     scalar2=None, op0=ALU.add), reads=[bb, bC], writes=[bC])
                          pg.op(DVE, lambda e: e.tensor_tensor(out=CU2[:, 0:NCMP], in0=CU[:, 0:NCMP], in1=CU[:, 0:NCMP], op=ALU.mult),
                                reads=[bC], writes=[bC])
                          pg.op(DVE, lambda e: e.tensor_scalar(out=CU2[:, 0:NCMP], in0=CU2[:, 0:NCMP], scalar1=0.044715, scalar2=1.0,
                                                               op0=ALU.mult, op1=ALU.add), reads=[bC], writes=[bC])
                          pg.op(DVE, lambda e: e.tensor_tensor(out=CU2[:, 0:NCMP], in0=CU2[:, 0:NCMP], in1=CU[:, 0:NCMP], op=ALU.mult),
                                reads=[bC], writes=[bC])
                          pg.op(ACT, lambda e: e.activation(out=CSG[:, 0:NCMP], in_=CU2[:, 0:NCMP], func=AF.Sigmoid,
                                                            scale=1.5957691216057308), reads=[bC], writes=[bC])
                          pg.op(DVE, lambda e: e.tensor_tensor(out=GT[:, 0:NCMP], in0=CU[:, 0:NCMP], in1=CSG[:, 0:NCMP], op=ALU.mult),
                                reads=[bC], writes=[bC])
                          T, o, bb = next_bank()
                          if kv == 0:
                              pg.op(PE, mm(T[:, o:o + NCMP], W2K[:, l, :], GT[:, 0:NCMP], True, True),
                                    reads=[bC, bPARAM], writes=[bb])
                              pg.op(ACT, lambda e, T=T, o=o: e.activation(out=KCZ[0:64, 0:NCMP], in_=T[0:64, o:o + NCMP], func=AF.Copy),
                                    reads=[bb], writes=[UB("KCT")])
                          else:
                              pg.op(PE, mm(T[0:NCMP, o:o + 64], GT[:, 0:NCMP], W2V[:, l, :], True, True),
                                    reads=[bC, bPARAM], writes=[bb])
                              pg.op(ACT, lambda e, T=T, o=o: e.activation(out=VCA[0:NCMP, 0:64], in_=T[0:NCMP, o:o + 64], func=AF.Copy),
                                    reads=[bb], writes=[bVCA])

                      chk('proj')
                      def pv_block(pt, bpt, vfn, bv, ncol, qts, first_flag):
                          OT, oo, ob = first_flag["bank"]
                          for qt in qts:
                              st_ = not first_flag["started"]
                              first_flag["started"] = True
                              pg.op(PE, lambda e, qt=qt, st_=st_: e.matmul(
                                  OT[:, oo + qt * 97:oo + qt * 97 + ncol], lhsT=pt[0:first_flag["nk"], qt * 128:(qt + 1) * 128],
                                  rhs=vfn, start=st_, stop=first_flag["last"], skip_group_check=True),
                                  reads=[bpt, bv], writes=[ob])

                      def finalize(ob_t, hh, qb, br, first_acc):
                          OT, oo, ob = ob_t
                          ov_ = OT[:, oo:oo + 4 * 97].rearrange("p (q c) -> p q c", q=4)
                          bF = UB("FIN")
                          pg.op(DVE, lambda e: e.tensor_scalar(out=RL[:, :], in0=ov_[:, :, 64], scalar1=1e-30, scalar2=None,
                                                               op0=ALU.max), reads=[ob], writes=[bF])
                          pg.op(DVE, lambda e: e.reciprocal(out=RL[:, :], in_=RL[:, :]), reads=[bF], writes=[bF])
                          pg.op(DVE, lambda e: e.tensor_tensor(out=CG[:, :], in0=RL[:, :], in1=GATES[:, qb * 4:qb * 4 + 4, hh * 3 + br],
                                                               op=ALU.mult), reads=[bF, UB("GATES")], writes=[bF])
                          acc = ACC[:, qb * 4:qb * 4 + 4, hh, :]
                          bacc = UB("ACC%d_%d" % (hh, qb))
                          cgb = CG[:, :].unsqueeze(2).broadcast_to([128, 4, 64])
                          if first_acc:
                              pg.op(DVE, lambda e: e.tensor_tensor(out=acc, in0=ov_[:, :, 0:64], in1=cgb, op=ALU.mult),
                                    reads=[ob, bF], writes=[bacc])
                          else:
                              pg.op(DVE, lambda e: e.tensor_tensor(out=FT[:, :, :], in0=ov_[:, :, 0:64], in1=cgb, op=ALU.mult),
                                    reads=[ob, bF], writes=[bF])
                              pg.op(POOL, lambda e: e.tensor_tensor(out=acc, in0=acc, in1=FT[:, :, :], op=ALU.add),
                                    reads=[bF, bacc], writes=[bacc])
                          if br == 0:
                              rlb = RL[:, :].unsqueeze(2).broadcast_to([128, 4, 32])
                              imp = IMP[:, qb * 4:qb * 4 + 4, :]
                              bimp = UB("IMP%d" % qb)
                              if hh == 0:
                                  pg.op(DVE, lambda e: e.tensor_tensor(out=imp, in0=ov_[:, :, 65:97], in1=rlb, op=ALU.mult),
                                        reads=[ob, bF], writes=[bimp])
                              else:
                                  pg.op(DVE, lambda e: e.tensor_tensor(out=FT2[:, :, :], in0=ov_[:, :, 65:97], in1=rlb, op=ALU.mult),
                                        reads=[ob, bF], writes=[bF])
                                  pg.op(DVE, lambda e: e.tensor_tensor(out=imp, in0=imp, in1=FT2[:, :, :], op=ALU.add),
                                        reads=[bF, bimp], writes=[bimp])

                      cbp = Pref([(cmpb_sc[4 * g + hh, :, qb * 512:(qb + 1) * 512], ("cmpb", 4 * g + hh))
                                  for hh in range(4) for qb in range(4)],
                                 [(CBT[0], UB("CBT0")), (CBT[1], UB("CBT1"))])
                      stpf = Pref([(strips_sc[4 * g + hh], ("strip", 4 * g + hh)) for hh in range(4)],
                                  [(STS[0], UB("ST0")), (STS[1], UB("ST1"))])
                      delta_done = set()

                      class Pair:
                          s1 = s2 = s3 = None

                      def hook(fn):
                          p = Pair()
                          p.s1 = fn
                          return p

                      def cmp_pair(hh, qb):
                          h = 4 * g + hh
                          j = hh // 2
                          r0 = 64 * (hh % 2)
                          p = Pair()
                          st_ = {}

                          def s1():
                              cb, bcb = cbp.get(hh * 4 + qb)
                              T, o, bb = next_bank()
                              st_["sc"] = (T, o, bb)
                              pg.op(PE, mm(T[0:NCMP, o:o + 512], KCZ[:, 0:NCMP], QH[hh][:, qb * 512:(qb + 1) * 512],
                                           True, False), reads=[UB("KCT"), bQH[hh]], writes=[bb])
                              pg.op(PE, mm(T[0:NCMP, o:o + 512], IDB[0:NCMP, 0:NCMP], cb[0:NCMP, :], False, True),
                                    reads=[bcb, bCB], writes=[bb])

                          def s2():
                              T, o, bb = st_["sc"]
                              pt = PTS[pt_i[0] % 4]; bpt = UB("PT%d" % (pt_i[0] % 4)); pt_i[0] += 1
                              st_["pt"] = (pt, bpt)
                              pg.op(ACT, lambda e: e.activation(out=pt[0:NCMP, :], in_=T[0:NCMP, o:o + 512], func=AF.Exp),
                                    reads=[bb], writes=[bpt])

                          def s3():
                              pt, bpt = st_["pt"]
                              ob_t = next_obank()
                              ff = {"bank": ob_t, "started": False, "last": True, "nk": NCMP}
                              pv_block(pt, bpt, VCA[0:NCMP, 0:97], bVCA, 97, range(4), ff)
                              finalize(ob_t, hh, qb, 0, True)

                          p.s1, p.s2, p.s3 = s1, s2, s3
                          return p

                      def mask_dve():
                          bimps = [UB("IMP%d" % qb) for qb in range(4)]
                          bSEL = UB("SEL")
                          pg.op(DVE, lambda e: e.tensor_tensor(out=SCORE[:, :, :], in0=IMP[:, :, :],
                                                               in1=ADDM.rearrange("p (q j) -> p q j", q=16), op=ALU.add),
                                reads=bimps + [bCF], writes=[bSEL])
                          for qt in range(16):
                              pg.op(DVE, lambda e, qt=qt: e.max(out=M8[:, qt, :], in_=SCORE[:, qt, :]), reads=[bSEL], writes=[bSEL])
                          pg.op(DVE, lambda e: e.tensor_tensor(out=NEGM[:, :, 64:96], in0=SCORE[:, :, :],
                                                               in1=M8[:, :, 7:8].broadcast_to([128, 16, 32]), op=ALU.is_ge),
                                reads=[bSEL], writes=[bSEL])
                          pg.op(DVE, lambda e: e.tensor_scalar(out=NEGM[:, :, 64:96], in0=NEGM[:, :, 64:96], scalar1=1.0, scalar2=-NEGB,
                                                               op0=ALU.subtract, op1=ALU.mult), reads=[bSEL], writes=[bSEL])

                      def mask_pe():
                          bSEL = UB("SEL")
                          for qb in range(4):
                              T, o, bb = next_bank()
                              for k in range(4):
                                  qt = qb * 4 + k
                                  pg.op(PE, lambda e, T=T, o=o, k=k, qt=qt: e.transpose(T[0:96, o + k * 128:o + (k + 1) * 128],
                                                                                      NEGM[:, qt, :], IDF),
                                        reads=[bSEL, bCF], writes=[bb])
                              for hh in range(4):
                                  dst = QH[hh][64:96, qb * 512:(qb + 1) * 512]
                                  if hh % 2 == 0:
                                      pg.op(ACT, lambda e, T=T, o=o, dst=dst: e.activation(out=dst, in_=T[64:96, o:o + 512], func=AF.Copy),
                                            reads=[bb], writes=[bQH[hh]])
                                  else:
                                      pg.op(DVE, lambda e, T=T, o=o, dst=dst: e.tensor_copy(out=dst, in_=T[64:96, o:o + 512]),
                                            reads=[bb], writes=[bQH[hh]])

                      def block_pairs(hh, qb, br):
                          h = 4 * g + hh
                          j = hh // 2
                          r0 = 64 * (hh % 2)
                          q0 = qb * 512
                          qsl = slice(q0, q0 + 512)
                          kts = list(range(0, 4 * qb + 4)) if br == 1 else list(range(max(0, 4 * qb - 4), 4 * qb + 4))
                          blk = {"ff": None}
                          KT_ = KSE if br == 1 else KWZ
                          bKT = UB("KST") if br == 1 else UB("KWT")
                          VT_ = VS if br == 1 else VW
                          bVT = UB("VS") if br == 1 else UB("VW")
                          out = []
                          for ki, kt in enumerate(kts):
                              p = Pair()
                              st_ = {}
                              k0 = kt * 128
                              d0 = q0 - k0
                              near = d0 <= 128
                              use_cut = (br == 2 and d0 >= 128)
                              use_mask = (br == 1)

                              qts = []
                              for qt in range(4):
                                  qlo, qhi = q0 + qt * 128, q0 + qt * 128 + 127
                                  if qhi < k0:
                                      continue
                                  if br == 2 and qlo - (k0 + 127) >= 512:
                                      continue
                                  qts.append(qt)
                              c0, c1 = min(qts) * 128, (max(qts) + 1) * 128

                              def s1(st_=st_, k0=k0, d0=d0, near=near, use_cut=use_cut, c0=c0, c1=c1):
                                  stp, bst = stpf.get(hh)
                                  if hh not in delta_done:
                                      delta_done.add(hh)
                                      pg.op(DVE, lambda e: e.tensor_scalar(out=stp[:, :], in0=stp[:, :], scalar1=B31[:, h:h + 1],
                                                                           scalar2=None, op0=ALU.subtract),
                                            reads=[bst, bPARAM], writes=[bst])
                                  T, o, bb = next_bank()
                                  st_["sc"] = (T, o, bb)
                                  nxt = []
                                  if near:
                                      se = min(c1, 240 - d0)
                                      nxt.append((IDB, stp[:, d0 + 384 + c0:d0 + 384 + se], [bCB, bst], c0, se))
                                  if use_cut:
                                      nxt.append((IDB, CUT[:, d0 - 128 + c0:d0 - 128 + c1], [bCB], c0, c1))
                                  pg.op(PE, mm(T[:, o + c0:o + c1], KT_[:, k0:k0 + 128], QH[hh][:, q0 + c0:q0 + c1], True, len(nxt) == 0),
                                        reads=[bKT, bQH[hh]], writes=[bb])
                                  for ni, (lt, rh, rb, a0, a1) in enumerate(nxt):
                                      pg.op(PE, mm(T[:, o + a0:o + a1], lt, rh, False, ni == len(nxt) - 1), reads=rb, writes=[bb])

                              def s2(st_=st_, near=near, c0=c0, c1=c1):
                                  T, o, bb = st_["sc"]
                                  pt = PTS[pt_i[0] % 4]; bpt = UB("PT%d" % (pt_i[0] % 4)); pt_i[0] += 1
                                  st_["pt"] = (pt, bpt)
                                  pg.op(ACT, lambda e: e.activation(out=pt[:, c0:c1], in_=T[:, o + c0:o + c1], func=AF.Exp),
                                        reads=[bb], writes=[bpt])

                              def s3(st_=st_, k0=k0, kt=kt, ki=ki, qts=qts):
                                  pt, bpt = st_["pt"]
                                  if blk["ff"] is None:
                                      blk["ff"] = {"bank": next_obank(), "started": False, "last": False, "nk": 128}
                                  ff = blk["ff"]
                                  ff["last"] = (ki == len(kts) - 1)
                                  pv_block(pt, bpt, VT_[:, kt, 0:65], bVT, 65, qts, ff)
                                  if ki == len(kts) - 1:
                                      finalize(ff["bank"], hh, qb, br, False)

                              p.s1, p.s2, p.s3 = s1, s2, s3
                              out.append(p)
                          return out

                      pairs = [cmp_pair(hh, qb) for hh in range(4) for qb in range(4)]
                      pairs.append(hook(mask_dve))
                      first = True
                      for hh in range(4):
                          for qb in range(4):
                              pairs += block_pairs(hh, qb, 2)
                              if first:
                                  pairs.append(hook(mask_pe))
                                  first = False
                              pairs += block_pairs(hh, qb, 1)
                      LOOK = 3
                      inflight = []
                      for p in pairs:
                          if p.s2 is None:
                              for d_ in inflight:
                                  d_.s2(); d_.s3()
                              inflight = []
                              p.s1()
                              continue
                          p.s1()
                          inflight.append(p)
                          if len(inflight) > LOOK:
                              d_ = inflight.pop(0)
                              d_.s2(); d_.s3()
                      for d_ in inflight:
                          d_.s2(); d_.s3()
                      chk('selwin')
                      for c in range(2):
                          for qb in range(4):
                              T, o, bb = next_bank()
                              for k in range(4):
                                  qt = qb * 4 + k
                                  pg.op(PE, lambda e, T=T, o=o, k=k, qt=qt, c=c: e.transpose(
                                      T[:, o + k * 128:o + (k + 1) * 128],
                                      ACC[:, qt, 2 * c:2 * c + 2, :].rearrange("p h d -> p (h d)"), IDF),
                                      reads=[UB("ACC%d_%d" % (2 * c, qb)), UB("ACC%d_%d" % (2 * c + 1, qb)), bCF], writes=[bb])
                              eng = ACT if (qb % 2 == 0) else DVE
                              dst = MIX[:, 2 + 2 * g + c, qb * 512:(qb + 1) * 512]
                              if eng is ACT:
                                  pg.op(ACT, lambda e, T=T, o=o, dst=dst: e.activation(out=dst, in_=T[:, o:o + 512], func=AF.Copy),
                                        reads=[bb], writes=[bMIX[2 + 2 * g + c][qb]])
                              else:
                                  pg.op(DVE, lambda e, T=T, o=o, dst=dst: e.tensor_copy(out=dst, in_=T[:, o:o + 512]),
                                        reads=[bb], writes=[bMIX[2 + 2 * g + c][qb]])
                  full_barrier()

                  if debug and s == 0 and li == 0 and stop_after is None:
                      pg.dma(SP, dbg_d[:, :], MIX[:, :, :].rearrange("p c t -> p (c t)"), reads=[b for c in bMIX for b in c])

                  chk('attn')
                  V3 = u3
                  Z, FACT = V3["Z"], V3["FACT"]
                  ZB = [V3["ZB0"], V3["ZB1"]]
                  ZQ = [V3["ZQ0"], V3["ZQ1"]]
                  SIL = [V3["SIL0"], V3["SIL1"]]
                  MEAN, RSTD, M2 = V3["MEAN"], V3["RSTD"], V3["M2"]
                  NT = [V3["NT0"], V3["NT1"]]
                  NX = [V3["NX0"], V3["NX1"]]
                  OST = [V3["OST0"], V3["OST1"]]
                  zq_i = [0]
                  nt_i = [0]
                  ost_i = [0]

                  ln_pending = []

                  def ln_pump(n=1):
                      for _ in range(n):
                          if ln_pending:
                              ln_pending.pop(0)()

                  def ln_flush():
                      while ln_pending:
                          ln_pending.pop(0)()

                  class LNStats:
                      def __init__(self):
                          self.Tm, self.om, self.bm = banks[6]
                          self.Tq, self.oq, self.bq = banks[7]
                          self.n = 0
                          self.pending = []

                      def push(self, ec):
                          self.pending.append(ec)
                          while len(self.pending) > 1:
                              self._emit(self.pending.pop(0))

                      def flush(self):
                          while self.pending:
                              self._emit(self.pending.pop(0))

                      def _emit(self, ec):
                          k = zq_i[0] % 2; zq_i[0] += 1
                          zb, bzb = ZB[k], UB("ZB%d" % k)
                          zq, bzq = ZQ[k], UB("ZQ%d" % k)
                          pg.op(ACT, lambda e: e.activation(out=zb[:, :], in_=Z[:, ec, :], func=AF.Copy), reads=[UB("Z%d" % ec)], writes=[bzb])
                          pg.op(ACT, lambda e: e.activation(out=zq[:, :], in_=Z[:, ec, :], func=AF.Square),
                                reads=[UB("Z%d" % ec)], writes=[bzq])
                          first, last = self.n == 0, self.n == 7
                          self.n += 1
                          pg.op(PE, mm(self.Tm[:, self.om:self.om + 512], ONESB, zb[:, :], first, last), reads=[bzb, bCB], writes=[self.bm])
                          pg.op(PE, mm(self.Tq[:, self.oq:self.oq + 512], ONESB, zq[:, :], first, last), reads=[bzq, bCB], writes=[self.bq])

                  def layer_norm(tb, gi, bi, final, stats):
                      tsl = slice(tb * 512, (tb + 1) * 512)
                      ln_flush()
                      stats.flush()
                      Tm, om, bm = stats.Tm, stats.om, stats.bm
                      Tq, oq, bq = stats.Tq, stats.oq, stats.bq
                      bS = UB("LNS")
                      pg.op(ACT, lambda e: e.activation(out=MEAN[:, :], in_=Tm[:, om:om + 512], func=AF.Copy), reads=[bm], writes=[bS])
                      pg.op(DVE, lambda e: e.tensor_tensor(out=M2[:, :], in0=MEAN[:, :], in1=MEAN[:, :], op=ALU.mult), reads=[bS], writes=[bS])
                      pg.op(DVE, lambda e: e.scalar_tensor_tensor(out=M2[:, :], in0=Tq[:, oq:oq + 512], scalar=LN_EPS, in1=M2[:, :],
                                                                  op0=ALU.add, op1=ALU.subtract), reads=[bq, bS], writes=[bS])
                      pg.op(ACT, lambda e: e.activation(out=M2[:, :], in_=M2[:, :], func=AF.Sqrt), reads=[bS], writes=[bS])
                      pg.op(DVE, lambda e: e.reciprocal(out=RSTD[:, :], in_=M2[:, :]), reads=[bS], writes=[bS])
                      def chunk_fn(ec):
                          nt = NT[nt_i[0] % 2]; bnt = UB("NT%d" % (nt_i[0] % 2))
                          nx = NX[nt_i[0] % 2]; bnx = UB("NX%d" % (nt_i[0] % 2)); nt_i[0] += 1
                          pg.op(DVE, lambda e: e.tensor_tensor(out=nt[:, :], in0=Z[:, ec, :], in1=MEAN[:, :], op=ALU.subtract),
                                reads=[UB("Z%d" % ec), bS], writes=[bnt])
                          pg.op(DVE, lambda e: e.tensor_tensor(out=nt[:, :], in0=nt[:, :], in1=RSTD[:, :], op=ALU.mult),
                                reads=[bnt, bS], writes=[bnt])
                          dstf = Z[:, ec, :] if final else nx[:, :]
                          bdst = UB("Z%d" % ec) if final else bnx
                          pg.op(ACT, lambda e: e.activation(
                              out=dstf, in_=nt[:, :], func=AF.Identity, scale=LNP[:, l, gi, ec:ec + 1], bias=LNP[:, l, bi, ec:ec + 1]),
                              reads=[bnt, bPARAM], writes=[bdst])
                          if not final:
                              pg.op(ACT, lambda e: e.activation(out=XH[:, ec, tsl], in_=nx[:, :], func=AF.Copy),
                                    reads=[bnx], writes=[bXH[ec][tb]])
                              pg.op(POOL, lambda e: e.tensor_tensor(out=XL[:, ec, tsl], in0=nx[:, :], in1=XH[:, ec, tsl],
                                                                    op=ALU.subtract),
                                    reads=[bnx, bXH[ec][tb]], writes=[bXL[ec][tb]])

                      def store_fn(k):
                          ot = OST[ost_i[0] % 2]; bot = UB("OST%d" % (ost_i[0] % 2)); ost_i[0] += 1
                          for hf in range(2):
                              T, o, bb = next_bank()
                              for jj in range(4):
                                  ec = hf * 4 + jj
                                  pg.op(PE, lambda e, jj=jj, ec=ec: e.transpose(
                                      T[:, o + jj * 128:o + (jj + 1) * 128], Z[:, ec, k * 128:(k + 1) * 128], IDF),
                                      reads=[UB("Z%d" % ec), bCF], writes=[bb])
                              if hf == 0:
                                  pg.op(ACT, lambda e: e.activation(out=ot[:, 0:512], in_=T[:, o:o + 512], func=AF.Copy),
                                        reads=[bb], writes=[bot])
                              else:
                                  pg.op(DVE, lambda e: e.tensor_copy(out=ot[:, 512:1024], in_=T[:, o:o + 512]),
                                        reads=[bb], writes=[bot])
                          r = tb * 512 + k * 128
                          pg.dma(SP, y_d[s, r:r + 128, :], ot[:, :], reads=[bot])

                      for ec in range(8):
                          ln_pending.append(lambda ec=ec: chunk_fn(ec))
                      if final:
                          for k in range(4):
                              ln_pending.append(lambda k=k: store_fn(k))

                  def residual_into_z(ec, tb, T, o, bb, stats):
                      tsl = slice(tb * 512, (tb + 1) * 512)
                      pg.op(DVE, lambda e: e.scalar_tensor_tensor(out=Z[:, ec, :], in0=XH[:, ec, tsl], scalar=ALPHA, in1=T[:, o:o + 512],
                                                                  op0=ALU.mult, op1=ALU.add),
                            reads=[bXH[ec][tb], bb], writes=[UB("Z%d" % ec)])
                      pg.op(DVE, lambda e: e.scalar_tensor_tensor(out=Z[:, ec, :], in0=XL[:, ec, tsl], scalar=ALPHA, in1=Z[:, ec, :],
                                                                   op0=ALU.mult, op1=ALU.add),
                            reads=[bXL[ec][tb], UB("Z%d" % ec)], writes=[UB("Z%d" % ec)])
                      stats.push(ec)

                  for tb in range(4):
                      tsl = slice(tb * 512, (tb + 1) * 512)
                      stats = LNStats()
                      for ec in range(8):
                          slab, bs = next_slab()
                          T, o, bb = next_bank()
                          for kc in range(8):
                              pg.op(PE, mm(T[:, o:o + 512], slab[:, kc * 128:(kc + 1) * 128], MIX[:, kc, tsl], kc == 0, kc == 7),
                                    reads=[bs, bMIX[kc][tb]], writes=[bb])
                          ln_pump(1)
                          residual_into_z(ec, tb, T, o, bb, stats)
                      layer_norm(tb, 0, 1, False, stats)
                  chk('phase3')
                  sil_i = [0]
                  for tb in range(4):
                      tsl = slice(tb * 512, (tb + 1) * 512)
                      for fc in range(NFC):
                          slab, bs = next_slab()
                          Tg, og, bg = next_bank()
                          Tu, ou, bu = next_bank()
                          for kc in range(8):
                              pg.op(PE, mm(Tg[:, og:og + 512], slab[:, kc * 256:kc * 256 + 128], XH[:, kc, tsl], kc == 0, kc == 7),
                                    reads=[bs, bXH[kc][tb]], writes=[bg])
                          for kc in range(8):
                              pg.op(PE, mm(Tu[:, ou:ou + 512], slab[:, kc * 256 + 128:kc * 256 + 256], XH[:, kc, tsl], kc == 0, kc == 7),
                                    reads=[bs, bXH[kc][tb]], writes=[bu])
                          sl = SIL[sil_i[0] % 2]; bsl = UB("SIL%d" % (sil_i[0] % 2)); sil_i[0] += 1
                          pg.op(ACT, lambda e, sl=sl, Tg=Tg, og=og: e.activation(out=sl[:, :], in_=Tg[:, og:og + 512], func=AF.Silu),
                                reads=[bg], writes=[bsl])
                          pg.op(DVE, lambda e, sl=sl, Tu=Tu, ou=ou, fc=fc: e.tensor_tensor(out=FACT[:, fc, :], in0=sl[:, :], in1=Tu[:, ou:ou + 512],
                                                                                       op=ALU.mult),
                                reads=[bsl, bu], writes=[UB("FACT%d" % fc)])
                          ln_pump(1)
                      stats = LNStats()
                      for ec in range(8):
                          T, o, bb = next_bank()
                          for half in range(2):
                              slab, bs = next_slab()
                              for f2 in range(NH):
                                  fc = half * NH + f2
                                  pg.op(PE, mm(T[:, o:o + 512], slab[:, f2 * 128:(f2 + 1) * 128], FACT[:, fc, :], fc == 0, fc == NFC - 1),
                                        reads=[bs, UB("FACT%d" % fc)], writes=[bb])
                          residual_into_z(ec, tb, T, o, bb, stats)
                      layer_norm(tb, 2, 3, last_layer, stats)
                  ln_flush()
                  full_barrier()


        except _Stop:
            full_barrier()
            if debug:
                pg.dma(SP, dbg_d[:, :], MIX[:, :, :].rearrange("p c t -> p (c t)"), reads=[b for c in bMIX for b in c])
        for E in (SP,):
            for sm in SP.ring:
                if sm.total > SP.known.get(sm, 0):
                    SP.h.wait_ge(sm.h, sm.total)
                    SP.known[sm] = sm.total
    return nc


def host_constants(rel_bias):
    rel_bias = np.asarray(rel_bias, np.float32)
    p = np.arange(128)[:, None]
    J = np.arange(1024)[None, :]
    d = J - 384 - p
    bk = t5_bucket_np(d)
    strips = np.empty((8, 128, 1024), np.float32)
    for h in range(8):
        strips[h] = np.where(d >= 0, rel_bias[bk, h], np.float32(NEGB))
    n = np.arange(128)[:, None]
    q = np.arange(S)[None, :]
    dc = q - (16 * n + 31)
    bkc = t5_bucket_np(dc)
    cmpb = np.empty((8, 128, S), np.float32)
    for h in range(8):
        cmpb[h] = np.where(dc >= 0, rel_bias[bkc, h], np.float32(NEGB))
    b31 = np.ascontiguousarray(np.broadcast_to(rel_bias[31][None, :], (128, 8))).astype(np.float32)
    cf = np.zeros((128, 128 + 128 + 512 + 32 + 4), np.float32)
    cf[:, 0:128] = np.eye(128, dtype=np.float32)
    cf[:, 128:256] = 1.0 / D
    t = (np.arange(16)[None, :] * 128 + np.arange(128)[:, None])
    jj = np.arange(32)
    cur = t // 64
    addm = np.zeros((128, 16, 32), np.float32)
    for j in jj:
        forced = (j == 0) | (j == cur) | (j == cur - 1)
        valid = (j * 64 <= t)
        addm[:, :, j] = np.where(valid, np.where(forced, 1e6, 0.0), -1e30)
    cf[:, 256:768] = addm.reshape(128, 512)
    tt = np.arange(16) + 1
    for ch in range(2):
        for hf in range(2):
            w = POOL_WINDOWS[2 * ch + hf]
            cf[hf * 64:(hf + 1) * 64, 768 + ch * 16:768 + (ch + 1) * 16] = 1.0 / np.minimum(tt, w)[None, :]
            cf[hf * 64:(hf + 1) * 64, 800 + ch] = 1.0 / w
    cb = np.zeros((128, 128 + 896 + 33 + S + 128), np.float32)
    cb[:, 1057 + S:] = 1.0 / D
    cb[:, 0:128] = np.eye(128, dtype=np.float32)
    Jc = np.arange(896)[None, :]
    dcut = Jc + 128 - p
    cb[:, 128:1024] = np.where(dcut >= 512, NEGB, 0.0)
    cb[:, 1024] = 1.0
    cs = np.arange(NCMP) * 16
    ss = np.arange(32) * 64
    ov = np.clip(np.minimum(cs[:, None] + 32, ss[None, :] + 64) - np.maximum(cs[:, None], ss[None, :]), 0, None) / 16.0
    cb[0:NCMP, 1025:1057] = ov
    k = np.arange(S)
    cb[64:96, 1057:1057 + S] = (k[None, :] // 64 == np.arange(32)[:, None]).astype(np.float32)
    return strips, cmpb, b31, cf, cb


def make_shared(w_in, conv_w, cmp_pe, cmp_w1, cmp_w2, pool_w, pool_scale, w_out,
                ln1_g, ln1_b, w_gate, w_up, w_down, ln2_g, ln2_b, rel_bias):
    f = lambda a: np.ascontiguousarray(np.asarray(a, np.float32))
    strips, cmpb, b31, cf, cb = host_constants(rel_bias)
    NL = 2
    conv_w = f(conv_w)
    convw_t = conv_w.reshape(NL, 3, 2, 128).transpose(3, 0, 2, 1).reshape(128, NL * 2 * 3)
    pet_t = f(cmp_pe).transpose(1, 3, 0, 2).reshape(128, NL * 32)
    lnp = np.stack([f(ln1_g), f(ln1_b), f(ln2_g), f(ln2_b)], axis=1)
    lnp_t = lnp.reshape(NL, 4, 8, 128).transpose(3, 0, 1, 2).reshape(128, NL * 4 * 8)
    pscale_t = f(pool_scale).reshape(NL, 2, 128).transpose(2, 0, 1).reshape(128, NL * 2)
    return {
        "w_in": f(w_in), "cmp_w1": f(cmp_w1), "cmp_w2": f(cmp_w2), "pool_w": f(pool_w), "w_out": f(w_out),
        "w_gate": f(w_gate), "w_up": f(w_up), "w_down": f(w_down),
        "convw_t": f(convw_t), "pet_t": f(pet_t), "lnp_t": f(lnp_t), "pscale_t": f(pscale_t),
        "strips": strips, "cmpb": cmpb, "b31": b31, "cf32": cf, "cbf": cb,
    }


_NC_CACHE = {}


def kernel(x, w_in, conv_w, cmp_pe, cmp_w1, cmp_w2, pool_w, pool_scale, w_out,
           ln1_g, ln1_b, w_gate, w_up, w_down, ln2_g, ln2_b, rel_bias):
    ncores = 8
    x = np.asarray(x, np.float32)
    nseq = x.shape[0] // ncores
    if "nc" not in _NC_CACHE:
        _NC_CACHE["nc"] = build_nc(nseq=nseq, layers=(0, 1))
    nc = _NC_CACHE["nc"]
    shared = make_shared(w_in, conv_w, cmp_pe, cmp_w1, cmp_w2, pool_w, pool_scale, w_out,
                         ln1_g, ln1_b, w_gate, w_up, w_down, ln2_g, ln2_b, rel_bias)
    in_maps = []
    for c in range(ncores):
        m = dict(shared)
        m["x"] = np.ascontiguousarray(x[c * nseq:(c + 1) * nseq])
        in_maps.append(m)
    res = run_bass_kernel_spmd(nc, in_maps, core_ids=list(range(ncores)))
    return np.concatenate([np.asarray(r["y"], np.float32) for r in res.results], axis=0)
```
